# Optimizing a Trainium2 kernel written in Bass

```python
import math
import numpy as np
import jax, jax.numpy as jnp
from jax import lax

D_MODEL = 1024
BATCH = 4
SEQ = 8192
DEPTH = 2

D_MIX = D_MODEL
RG_WIDTH = D_MIX // 2
RG_BLOCKS = 8
RG_BLOCK = RG_WIDTH // RG_BLOCKS
CONV_W = 4
RG_C = 8.0
A_MIN = 0.9
A_MAX = 0.999

NSA_WIDTH = D_MIX - RG_WIDTH
HEAD_DIM = 64
N_HEADS = NSA_WIDTH // HEAD_DIM
N_KV = 2
HPG = N_HEADS // N_KV
KV_WIDTH = N_KV * HEAD_DIM
ROPE_DIM = HEAD_DIM // 4
ROPE_THETA = 500000.0
CMP_LEN = 32
CMP_STRIDE = 16
CMP_HIDDEN = 2 * HEAD_DIM
SEL_LEN = 64
SEL_TOPN = 16
WINDOW = 512
N_BRANCH = 3
Q_BLOCK = 128
EPS = 1e-6
NEG = -1e30
BIG = 1e30
N_IN = 2 * RG_WIDTH + 2 * NSA_WIDTH + 6 * KV_WIDTH + N_BRANCH * N_HEADS

kernel_name = "hybrid_rglru_nsa_parallel_heads"


def rms_norm(x, g):
    xf = x.astype(jnp.float32)
    y = xf * lax.rsqrt(jnp.mean(xf * xf, axis=-1, keepdims=True) + EPS)
    return (y * g.astype(jnp.float32)).astype(x.dtype)


def rope_partial(x, pos):
    half = ROPE_DIM // 2
    inv = ROPE_THETA ** (-jnp.arange(half, dtype=jnp.float32) / half)
    ang = pos[:, None] * inv[None, :]
    cos, sin = jnp.cos(ang), jnp.sin(ang)
    xf = x.astype(jnp.float32)
    x1, x2, xp = xf[..., :half], xf[..., half:ROPE_DIM], xf[..., ROPE_DIM:]
    out = jnp.concatenate([x1 * cos - x2 * sin, x2 * cos + x1 * sin, xp], axis=-1)
    return out.astype(x.dtype)


def masked_softmax(s, mask):
    s = jnp.where(mask, s.astype(jnp.float32), NEG)
    return jax.nn.softmax(s, axis=-1) * mask


def causal_depthwise_conv(x, w, b):
    S = x.shape[1]
    xp = jnp.pad(x, ((0, 0), (CONV_W - 1, 0), (0, 0)))
    y = sum(xp[:, k:k + S] * w[k] for k in range(CONV_W))
    return y + b


def rg_lru(x, w_r, b_r, w_i, b_i, lam):
    B, S, C = x.shape
    xb = x.reshape(B, S, RG_BLOCKS, RG_BLOCK)
    r = jax.nn.sigmoid(jnp.einsum('bsnc,ncd->bsnd', xb, w_r).reshape(B, S, C) + b_r)
    i = jax.nn.sigmoid(jnp.einsum('bsnc,ncd->bsnd', xb, w_i).reshape(B, S, C) + b_i)
    log_a = (-RG_C * r.astype(jnp.float32)) * jax.nn.softplus(-lam.astype(jnp.float32))
    a = jnp.exp(log_a)
    u = jnp.sqrt(-jnp.expm1(2.0 * log_a)) * (i * x).astype(jnp.float32)

    def combine(left, right):
        a1, b1 = left
        a2, b2 = right
        return a1 * a2, a2 * b1 + b2

    _, h = lax.associative_scan(combine, (a, u), axis=1)
    return h.astype(x.dtype)


def compress_blocks(blocks, pe, w1, w2):
    z = (blocks + pe).reshape(blocks.shape[:-2] + (CMP_LEN * HEAD_DIM,))
    return jax.nn.silu(z @ w1) @ w2


def nsa(q, k_c, v_c, k_s, v_s, k_w, v_w, gates, q_g, k_g,
        pe_k, w1_k, w2_k, pe_v, w1_v, w2_v):
    B, S, _ = q.shape
    pos = jnp.arange(S, dtype=jnp.float32)
    scale = 1.0 / math.sqrt(HEAD_DIM)
    n_cmp = (S - CMP_LEN) // CMP_STRIDE + 1
    n_sel = S // SEL_LEN
    top_n = min(SEL_TOPN, n_sel)
    n_qb = S // Q_BLOCK

    def kv_heads(t):
        return t.reshape(B, S, N_KV, HEAD_DIM).transpose(0, 2, 1, 3)

    qh = q.reshape(B, S, N_KV, HPG, HEAD_DIM).transpose(0, 2, 3, 1, 4)
    qh = rope_partial(rms_norm(qh, q_g), pos)

    blk_idx = np.arange(n_cmp)[:, None] * CMP_STRIDE + np.arange(CMP_LEN)[None, :]
    kc = compress_blocks(kv_heads(k_c)[:, :, blk_idx], pe_k, w1_k, w2_k)
    vc = compress_blocks(kv_heads(v_c)[:, :, blk_idx], pe_v, w1_v, w2_v)
    cmp_end = jnp.arange(n_cmp) * CMP_STRIDE + (CMP_LEN - 1)
    kc = rope_partial(rms_norm(kc, k_g[0]), cmp_end.astype(jnp.float32))

    cs = np.arange(n_cmp)[:, None] * CMP_STRIDE
    ss = np.arange(n_sel)[None, :] * SEL_LEN
    ov = np.maximum(0, np.minimum(cs + CMP_LEN, ss + SEL_LEN) - np.maximum(cs, ss))
    overlap = jnp.asarray((ov / CMP_LEN).astype(np.float32))

    ks = rope_partial(rms_norm(kv_heads(k_s), k_g[1]), pos)
    ks_blk = ks.reshape(B, N_KV, n_sel, SEL_LEN * HEAD_DIM)
    vs_blk = kv_heads(v_s).reshape(B, N_KV, n_sel, SEL_LEN * HEAD_DIM)

    kw = rope_partial(rms_norm(kv_heads(k_w), k_g[2]), pos)
    kw_pad = jnp.pad(kw, ((0, 0), (0, 0), (WINDOW, 0), (0, 0)))
    vw_pad = jnp.pad(kv_heads(v_w), ((0, 0), (0, 0), (WINDOW, 0), (0, 0)))

    g = jax.nn.sigmoid(gates).reshape(B, S, N_KV, HPG, N_BRANCH).transpose(0, 2, 3, 1, 4)
    j_sel = jnp.arange(n_sel)

    def q_block(qi):
        q0 = qi * Q_BLOCK
        t = q0 + jnp.arange(Q_BLOCK)
        qb = lax.dynamic_slice_in_dim(qh, q0, Q_BLOCK, axis=3)
        gb = lax.dynamic_slice_in_dim(g, q0, Q_BLOCK, axis=3)
        s_c = jnp.einsum('bghqd,bgkd->bghqk', qb, kc) * scale
        p_c = masked_softmax(s_c, cmp_end[None, :] <= t[:, None])
        o_c = jnp.einsum('bghqk,bgkd->bghqd', p_c.astype(vc.dtype), vc)
        imp = jnp.einsum('bghqk,kj->bgqj', p_c, overlap)
        bt = (t // SEL_LEN)[:, None]
        force = (j_sel[None, :] == 0) | (j_sel[None, :] == bt) | (j_sel[None, :] == bt - 1)
        imp = jnp.where(force, BIG, jnp.where(j_sel[None, :] <= bt, imp, NEG))
        _, idx = lax.top_k(imp, top_n)
        flat = idx.reshape(B, N_KV, Q_BLOCK * top_n)[..., None]
        kg = jnp.take_along_axis(ks_blk, flat, axis=2).reshape(
            B, N_KV, Q_BLOCK, top_n * SEL_LEN, HEAD_DIM)
        vg = jnp.take_along_axis(vs_blk, flat, axis=2).reshape(
            B, N_KV, Q_BLOCK, top_n * SEL_LEN, HEAD_DIM)
        kpos = (idx[..., None] * SEL_LEN + jnp.arange(SEL_LEN)).reshape(
            B, N_KV, Q_BLOCK, top_n * SEL_LEN)
        m_s = (kpos <= t[None, None, :, None])[:, :, None]
        s_s = jnp.einsum('bghqd,bgqkd->bghqk', qb, kg) * scale
        p_s = masked_softmax(s_s, m_s)
        o_s = jnp.einsum('bghqk,bgqkd->bghqd', p_s.astype(vg.dtype), vg)
        kwb = lax.dynamic_slice_in_dim(kw_pad, q0, WINDOW + Q_BLOCK, axis=2)
        vwb = lax.dynamic_slice_in_dim(vw_pad, q0, WINDOW + Q_BLOCK, axis=2)
        kp = (q0 - WINDOW + jnp.arange(WINDOW + Q_BLOCK))[None, :]
        m_w = (kp <= t[:, None]) & (kp > t[:, None] - WINDOW) & (kp >= 0)
        s_w = jnp.einsum('bghqd,bgkd->bghqk', qb, kwb) * scale
        p_w = masked_softmax(s_w, m_w)
        o_w = jnp.einsum('bghqk,bgkd->bghqd', p_w.astype(vwb.dtype), vwb)
        return gb[..., 0:1] * o_c + gb[..., 1:2] * o_s + gb[..., 2:3] * o_w

    outs = lax.map(q_block, jnp.arange(n_qb))
    return outs.transpose(1, 0, 4, 2, 3, 5).reshape(B, S, NSA_WIDTH)


def hybrid_layer(x, norm_g, w_in, conv_w, conv_b, rg_wr, rg_br, rg_wi, rg_bi, rg_lam,
                 q_g, k_g, pe_k, w1_k, w2_k, pe_v, w1_v, w2_v, w_out):
    h = rms_norm(x, norm_g)
    z = h @ w_in
    sizes = [RG_WIDTH, RG_WIDTH, NSA_WIDTH] + [KV_WIDTH] * 6 + [NSA_WIDTH]
    cuts = list(np.cumsum(sizes))
    (rg_x, rg_gate, q, k_c, v_c, k_s, v_s, k_w, v_w, nsa_gate, br_gate) = jnp.split(
        z, cuts, axis=-1)
    y_a = rg_lru(causal_depthwise_conv(rg_x, conv_w, conv_b), rg_wr, rg_br, rg_wi, rg_bi, rg_lam)
    y_a = y_a * jax.nn.silu(rg_gate)
    y_b = nsa(q, k_c, v_c, k_s, v_s, k_w, v_w, br_gate, q_g, k_g,
              pe_k, w1_k, w2_k, pe_v, w1_v, w2_v)
    y_b = y_b * jax.nn.silu(nsa_gate)
    y = jnp.concatenate([y_a, y_b], axis=-1)
    return x + y @ w_out


def setup_inputs(seed: int = 0) -> dict:
    key = jax.random.key(seed)
    ks = jax.random.split(key, 20)
    f32 = jnp.float32
    n = jax.random.normal
    a0 = jax.random.uniform(ks[10], (DEPTH, RG_WIDTH), f32, A_MIN, A_MAX)
    s0 = a0 ** (1.0 / RG_C)
    return {
        "x": n(ks[0], (BATCH, SEQ, D_MODEL), f32),
        "norm_g": 1.0 + 0.02 * n(ks[1], (DEPTH, D_MODEL), f32),
        "w_in": n(ks[2], (DEPTH, D_MODEL, N_IN), f32) * D_MODEL ** -0.5,
        "conv_w": n(ks[3], (DEPTH, CONV_W, RG_WIDTH), f32) * CONV_W ** -0.5,
        "conv_b": 0.01 * n(ks[4], (DEPTH, RG_WIDTH), f32),
        "rg_wr": n(ks[5], (DEPTH, RG_BLOCKS, RG_BLOCK, RG_BLOCK), f32) * RG_BLOCK ** -0.5,
        "rg_br": 0.01 * n(ks[6], (DEPTH, RG_WIDTH), f32),
        "rg_wi": n(ks[7], (DEPTH, RG_BLOCKS, RG_BLOCK, RG_BLOCK), f32) * RG_BLOCK ** -0.5,
        "rg_bi": 0.01 * n(ks[8], (DEPTH, RG_WIDTH), f32),
        "rg_lambda": jnp.log(s0) - jnp.log1p(-s0),
        "q_norm_g": 1.0 + 0.02 * n(ks[9], (DEPTH, HEAD_DIM), f32),
        "k_norm_g": 1.0 + 0.02 * n(ks[11], (DEPTH, N_BRANCH, HEAD_DIM), f32),
        "cmp_pe_k": 0.02 * n(ks[12], (DEPTH, CMP_LEN, HEAD_DIM), f32),
        "cmp_w1_k": n(ks[13], (DEPTH, CMP_LEN * HEAD_DIM, CMP_HIDDEN), f32) * (CMP_LEN * HEAD_DIM) ** -0.5,
        "cmp_w2_k": n(ks[14], (DEPTH, CMP_HIDDEN, HEAD_DIM), f32) * CMP_HIDDEN ** -0.5,
        "cmp_pe_v": 0.02 * n(ks[15], (DEPTH, CMP_LEN, HEAD_DIM), f32),
        "cmp_w1_v": n(ks[16], (DEPTH, CMP_LEN * HEAD_DIM, CMP_HIDDEN), f32) * (CMP_LEN * HEAD_DIM) ** -0.5,
        "cmp_w2_v": n(ks[17], (DEPTH, CMP_HIDDEN, HEAD_DIM), f32) * CMP_HIDDEN ** -0.5,
        "w_out": n(ks[18], (DEPTH, D_MIX, D_MODEL), f32) * D_MIX ** -0.5,
    }


def reference(x, norm_g, w_in, conv_w, conv_b, rg_wr, rg_br, rg_wi, rg_bi, rg_lambda,
              q_norm_g, k_norm_g, cmp_pe_k, cmp_w1_k, cmp_w2_k, cmp_pe_v, cmp_w1_v,
              cmp_w2_v, w_out):
    for l in range(DEPTH):
        x = hybrid_layer(x, norm_g[l], w_in[l], conv_w[l], conv_b[l], rg_wr[l], rg_br[l],
                         rg_wi[l], rg_bi[l], rg_lambda[l], q_norm_g[l], k_norm_g[l],
                         cmp_pe_k[l], cmp_w1_k[l], cmp_w2_k[l], cmp_pe_v[l], cmp_w1_v[l],
                         cmp_w2_v[l], w_out[l])
    return x
```

```python
import math
from contextlib import ExitStack

import numpy as np
import ml_dtypes
import concourse.bass as bass
import concourse.mybir as mybir
from concourse.bass_utils import run_bass_kernel_spmd

F32 = mybir.dt.float32
BF16 = mybir.dt.bfloat16
AF = mybir.ActivationFunctionType
ALU = mybir.AluOpType

COMPUTE = ("tensor", "vector", "scalar", "gpsimd")
ALLENG = ("tensor", "vector", "scalar", "gpsimd", "sync")


class Op:
    __slots__ = ("eng", "fn", "deps", "signal", "dma_sem", "cnt")

    def __init__(self, eng, fn, dma_sem):
        self.eng = eng
        self.fn = fn
        self.deps = []
        self.signal = False
        self.dma_sem = dma_sem
        self.cnt = 0


class Prog:
    def __init__(self, nc, stack):
        self.nc = nc
        self.stack = stack
        self.eng_ops = {e: [] for e in ALLENG}
        self.last_w = {}
        self.readers = {}
        self.sems = {}
        self.dma_cnt = {}
        self.last_dma = {}
        self.last_compute = {}
        self.same_engine_sync = True

    def sem(self, name):
        if name not in self.sems:
            self.sems[name] = self.stack.enter_context(self.nc.semaphore(name))
            self.dma_cnt[name] = 0
        return name

    def add(self, eng, fn, reads=(), writes=(), dma=None):
        if dma is not None:
            self.sem(dma)
        op = Op(eng, fn, dma)
        deps = []
        for b in reads:
            w = self.last_w.get(b)
            if w is not None:
                deps.append(w)
        for b in writes:
            w = self.last_w.get(b)
            if w is not None:
                deps.append(w)
            deps.extend(self.readers.get(b, ()))
        seen = set()
        for d in deps:
            if id(d) in seen or d is op:
                continue
            seen.add(id(d))
            if d.dma_sem is None and d.eng == eng:
                if eng == "tensor" or not self.same_engine_sync:
                    continue
            op.deps.append(d)
        for b in reads:
            self.readers.setdefault(b, []).append(op)
        for b in writes:
            self.last_w[b] = op
            self.readers[b] = []
        if dma is not None:
            self.dma_cnt[dma] += 16
            op.cnt = self.dma_cnt[dma]
            self.last_dma[dma] = op
        else:
            self.last_compute[eng] = op
        self.eng_ops[eng].append(op)
        return op

    def barrier(self):
        deps = list(self.last_compute.values()) + list(self.last_dma.values())
        for e in ALLENG:
            op = Op(e, None, None)
            op.deps = list(deps)
            self.eng_ops[e].append(op)
        self.last_w = {}
        self.readers = {}

    def emit(self):
        nc = self.nc
        for e in COMPUTE:
            self.sem("eng_" + e)
        for ops in self.eng_ops.values():
            for op in ops:
                for d in op.deps:
                    d.signal = True
        for e in ALLENG:
            c = 0
            for op in self.eng_ops[e]:
                if op.dma_sem is None and op.fn is not None:
                    if op.signal:
                        c += 1
                    op.cnt = c
        sems = self.sems
        eng_ops = self.eng_ops

        def run(engname, eng):
            waited = {}
            for op in eng_ops[engname]:
                need = {}
                for d in op.deps:
                    s = d.dma_sem if d.dma_sem is not None else "eng_" + d.eng
                    if d.cnt > need.get(s, 0):
                        need[s] = d.cnt
                for s, v in need.items():
                    if waited.get(s, 0) >= v:
                        continue
                    eng.wait_ge(sems[s], v)
                    waited[s] = v
                if op.fn is None:
                    continue
                ins = op.fn(eng)
                if op.dma_sem is not None:
                    ins.then_inc(sems[op.dma_sem], 16)
                elif op.signal:
                    ins.then_inc(sems["eng_" + engname], 1)

        with nc.Block() as block:
            @block.sync
            def _(e):
                run("sync", e)

            @block.tensor
            def _(e):
                run("tensor", e)

            @block.vector
            def _(e):
                run("vector", e)

            @block.scalar
            def _(e):
                run("scalar", e)

            @block.gpsimd
            def _(e):
                run("gpsimd", e)


NPAR = 32
EPS = 1e-6
NEGM = -32768.0
HD = 64


def bf(a):
    return np.ascontiguousarray(np.asarray(a, dtype=np.float32).astype(ml_dtypes.bfloat16))


def f32(a):
    return np.ascontiguousarray(np.asarray(a, dtype=np.float32))


def rope_tables(pos):
    half = 8
    inv = (np.float32(500000.0) ** (-np.arange(half, dtype=np.float32) / np.float32(half))).astype(np.float32)
    ang = pos.astype(np.float32)[None, :] * inv[:, None]
    c8 = np.cos(ang).astype(np.float32)
    s8 = np.sin(ang).astype(np.float32)
    T = len(pos)
    C = np.ones((64, T), np.float32)
    Sn = np.zeros((64, T), np.float32)
    C[0:8] = c8
    C[8:16] = c8
    Sn[0:8] = s8
    Sn[8:16] = s8
    return np.concatenate([C, C], 0), np.concatenate([Sn, Sn], 0)


def make_consts(S):
    NCMP = (S - 32) // 16 + 1
    NCT = (NCMP + 127) // 128
    c = {}
    c["ident_bf"] = bf(np.eye(128))
    c["ident_f"] = f32(np.eye(128))
    RT = np.zeros((128, 128), np.float32)
    for blk in (0, 64):
        for d in range(8):
            RT[blk + d + 8, blk + d] = -1.0
        for d in range(8, 16):
            RT[blk + d - 8, blk + d] = 1.0
    c["rotT"] = RT
    BO = np.zeros((128, 128), np.float32)
    BO[0:64, 0:64] = 1.0 / 64
    BO[64:128, 64:128] = 1.0 / 64
    c["blkones"] = BO
    c["cosT"], c["sinT"] = rope_tables(np.arange(S))
    cpos = np.arange(512) * 16 + 31
    c["cosC"], c["sinC"] = rope_tables(cpos)
    EE = np.zeros((128, S), np.float32)
    cc = np.arange(S)
    EE[cc // 64 % 128, cc] = 1.0
    c["EE"] = bf(EE)
    OV = np.zeros((128, NCT, 128), np.float32)
    for n in range(NCMP):
        for j in range(min(128, S // 64)):
            ov = max(0, min(16 * n + 32, 64 * j + 64) - max(16 * n, 64 * j))
            if ov:
                OV[n % 128, n // 128, j] = ov / 32.0
    c["OV"] = bf(OV)
    m = np.arange(128)[:, None]
    tl = np.arange(128)[None, :]
    tric = np.where(m <= tl, 0.0, NEGM).astype(np.float32)
    triw = np.where(m > tl, 0.0, NEGM).astype(np.float32)
    c["TRIC"] = bf(np.concatenate([tric, tric], 1))
    c["TRIW"] = bf(np.concatenate([triw, triw], 1))
    CM = np.zeros((128, 17, 256), np.float32)
    for r in range(17):
        mk = np.where(16 * m - tl <= 128 * r - 31, 0.0, NEGM)
        CM[:, r, 0:128] = mk
        CM[:, r, 128:256] = mk
    c["CMASK"] = bf(CM)
    FP = np.zeros((128, 256), np.float32)
    for t in range(128):
        hi = 1 if t >= 64 else 0
        FP[t, 128 + hi] = 2e30
        FP[t, 127 + hi] = 1e30
    c["FORCEP"] = FP
    return c


CONST_SPECS = [
    ("ident_bf", lambda S, N: [128, 128], BF16),
    ("ident_f", lambda S, N: [128, 128], F32),
    ("rotT", lambda S, N: [128, 128], F32),
    ("blkones", lambda S, N: [128, 128], F32),
    ("cosT", lambda S, N: [128, S], F32),
    ("sinT", lambda S, N: [128, S], F32),
    ("cosC", lambda S, N: [128, 512], F32),
    ("sinC", lambda S, N: [128, 512], F32),
    ("EE", lambda S, N: [128, S], BF16),
    ("OV", lambda S, N: [128, N, 128], BF16),
    ("TRIC", lambda S, N: [128, 256], BF16),
    ("TRIW", lambda S, N: [128, 256], BF16),
    ("CMASK", lambda S, N: [128, 17, 256], BF16),
    ("FORCEP", lambda S, N: [128, 256], F32),
]

WEIGHT_SPECS = [
    ("win_fm", [128, 8, 1152]),
    ("win_tm", [128, 8, 396]),
    ("params", [128, NPAR]),
    ("wrbd", [128, 2, 128]),
    ("wibd", [128, 2, 128]),
    ("w1kv", [128, 32, 128]),
    ("w2k2", [128, 128]),
    ("w2v", [128, 64]),
    ("pekv", [128, 32]),
]


def prep_layer(inp, l, g):
    w_in = inp["w_in"][l]
    o_rgx, o_rgg, o_q, o_kc, o_vc, o_ks, o_vs, o_kw, o_vw, o_ng, o_bg = (
        0, 512, 1024, 1536, 1664, 1792, 1920, 2048, 2176, 2304, 2816)
    ar = np.arange
    qh = lambda h: o_q + (4 * g + h) * 64 + ar(64)
    fm_cols = np.concatenate([
        o_rgx + 256 * g + ar(256),
        o_rgg + 256 * g + ar(256),
        qh(0), qh(2), qh(1), qh(3),
        o_ks + 64 * g + ar(64), o_ks + 64 * g + ar(64),
        o_kw + 64 * g + ar(64), o_kw + 64 * g + ar(64),
        o_kc + 64 * g + ar(64), o_vc + 64 * g + ar(64),
    ])
    tm_cols = np.concatenate([
        o_vs + 64 * g + ar(64), o_vw + 64 * g + ar(64),
        o_ng + 256 * g + ar(256), o_bg + 12 * g + ar(12),
    ])
    d = {}
    d["win_fm"] = f32(w_in[:, fm_cols].reshape(8, 128, 1152).transpose(1, 0, 2))
    d["win_tm"] = f32(w_in[:, tm_cols].reshape(8, 128, 396).transpose(1, 0, 2))
    par = np.zeros((128, NPAR), np.float32)
    par[:, 0:8] = inp["norm_g"][l].reshape(8, 128).T
    for c in range(2):
        ch = 256 * g + 128 * c + ar(128)
        par[:, 8 + 4 * c:12 + 4 * c] = inp["conv_w"][l][:, ch].T
        par[:, 16 + c] = inp["conv_b"][l][ch]
        par[:, 18 + c] = inp["rg_br"][l][ch]
        par[:, 20 + c] = inp["rg_bi"][l][ch]
        par[:, 22 + c] = inp["rg_lambda"][l][ch]
    d64 = ar(128) % 64
    par[:, 24] = inp["q_norm_g"][l][d64]
    par[:, 25] = inp["k_norm_g"][l][1][d64]
    par[:, 26] = inp["k_norm_g"][l][2][d64]
    par[:, 27] = inp["k_norm_g"][l][0][d64]
    d["params"] = par
    for nm, src in (("wrbd", "rg_wr"), ("wibd", "rg_wi")):
        w = np.zeros((128, 2, 128), np.float32)
        for c in range(2):
            for e in range(2):
                w[64 * e:64 * e + 64, c, 64 * e:64 * e + 64] = inp[src][l][4 * g + 2 * c + e]
        d[nm] = w
    w1 = np.zeros((128, 32, 128), np.float32)
    w1[0:64] = inp["cmp_w1_k"][l].reshape(32, 64, 128).transpose(1, 0, 2)
    w1[64:128] = inp["cmp_w1_v"][l].reshape(32, 64, 128).transpose(1, 0, 2)
    d["w1kv"] = w1
    d["w2k2"] = f32(np.concatenate([inp["cmp_w2_k"][l], inp["cmp_w2_k"][l]], 1))
    d["w2v"] = f32(inp["cmp_w2_v"][l])
    pe = np.zeros((128, 32), np.float32)
    pe[0:64] = inp["cmp_pe_k"][l].T
    pe[64:128] = inp["cmp_pe_v"][l].T
    d["pekv"] = pe
    return d


def prep_wout(inp, l):
    w = inp["w_out"][l]
    rows = []
    for g in range(2):
        rows += [256 * g + np.arange(128), 256 * g + 128 + np.arange(128),
                 512 + 256 * g + np.arange(128), 512 + 256 * g + 128 + np.arange(128)]
    rows = np.concatenate(rows)
    return f32(w[rows].reshape(8, 128, 1024).transpose(1, 0, 2))


def build_A(S, upto=4, dbg=False):
    NT = S // 128
    NG = S // 512
    NCMP = (S - 32) // 16 + 1
    NCT = (NCMP + 127) // 128
    nc = bass.Bass("TRN2", target_bir_lowering=False)
    dbg_out = {}
    with ExitStack() as stack:
        P = Prog(nc, stack)
        A = P.add

        def dram(name, shape, dt, kind):
            return nc.dram_tensor(name, list(shape), dt, kind=kind).ap()

        x = dram("x", [S, 1024], F32, "ExternalInput")
        W = {n: dram(n, shp, F32, "ExternalInput") for n, shp in WEIGHT_SPECS}
        C = {n: dram(n, fn(S, NCT), dt, "ExternalInput") for n, fn, dt in CONST_SPECS}
        yT = dram("yT", [4, 128, S], BF16, "ExternalOutput")
        skind = "ExternalOutput" if dbg else "Internal"
        rgx_s = dram("rgx_s", [2, 128, S], F32, skind)
        rgg_s = dram("rgg_s", [2, 128, S], F32, skind)
        sgn_s = dram("sgn_s", [S, 256], F32, skind)

        def sb(name, shape, dt, st=stack):
            return st.enter_context(nc.sbuf_tensor(name, list(shape), dt))

        par = sb("par", [128, NPAR], F32)
        identb = sb("identb", [128, 128], BF16)
        identf = sb("identf", [128, 128], F32)
        rotT = sb("rotT_sb", [128, 128], F32)
        blk1 = sb("blk1", [128, 128], F32)
        qT = sb("qT", [128, 2, S], BF16)
        KS2 = sb("KS2", [128, S], BF16)
        KW2 = sb("KW2", [128, S], BF16)
        KCV = sb("KCV", [128, S], BF16)
        VS = sb("VS", [128, NT, 65], BF16)
        VW = sb("VW", [128, NT, 65], BF16)
        gsig = sb("gsig", [128, NT, 12], F32)
        KC2 = sb("KC2", [128, 512], BF16)
        VC = sb("VC", [128, NCT, 65], BF16)
        nzf = sb("nzf", [128, 512], F32)
        nsq = sb("nsq", [128, 512], F32)
        nln = sb("nln", [128, 512], F32)
        nrs = sb("nrs", [128, 512], F32)
        nqn = sb("nqn", [128, 512], F32)
        nt1 = sb("nt1", [128, 512], F32)
        nt2 = sb("nt2", [128, 512], F32)
        psb = [stack.enter_context(nc.psum_tensor(f"psb{i}", [128, 512], F32)) for i in range(8)]

        for nm, t in (("params", par), ("ident_bf", identb), ("ident_f", identf), ("rotT", rotT), ("blkones", blk1)):
            src = W[nm] if nm == "params" else C[nm]
            A("sync", lambda e, t=t, src=src: e.dma_start(out=t[:], in_=src), writes=[t.name], dma="ld_" + t.name)
        A("gpsimd", lambda e: e.memset(VS[:, :, 64:65], 1.0), writes=["VS"])
        A("gpsimd", lambda e: e.memset(VW[:, :, 64:65], 1.0), writes=["VW"])
        A("gpsimd", lambda e: e.memset(VC[:, :, 64:65], 1.0), writes=["VC"])
        A("gpsimd", lambda e: e.memset(KC2[:], 0.0), writes=["KC2"])

        def normrope(ps_in, N, gcol, cos_ap, sin_ap, cos_keys, out_ap, out_key, in_key, ps_m, ps_r, km, kr):
            A("scalar", lambda e: e.activation(out=nzf[:, 0:N], in_=ps_in, func=AF.Copy), reads=[in_key], writes=["nzf"])
            A("scalar", lambda e: e.activation(out=nsq[:, 0:N], in_=ps_in, func=AF.Square), reads=[in_key], writes=["nsq"])
            A("tensor", lambda e: e.matmul(ps_m[:, 0:N], lhsT=blk1[:], rhs=nsq[:, 0:N], start=True, stop=True),
              reads=["nsq", "blk1"], writes=[km])
            A("scalar", lambda e: e.activation(out=nln[:, 0:N], in_=ps_m[:, 0:N], func=AF.Ln, bias=epsb[:, 0:1], scale=1.0),
              reads=[km, "epsb"], writes=["nln"])
            A("scalar", lambda e: e.activation(out=nrs[:, 0:N], in_=nln[:, 0:N], func=AF.Exp, scale=-0.5),
              reads=["nln"], writes=["nrs"])
            A("vector", lambda e: e.scalar_tensor_tensor(out=nqn[:, 0:N], in0=nzf[:, 0:N], scalar=par[:, gcol:gcol + 1],
                                                         in1=nrs[:, 0:N], op0=ALU.mult, op1=ALU.mult),
              reads=["nzf", "nrs", "par"], writes=["nqn"])
            A("tensor", lambda e: e.matmul(ps_r[:, 0:N], lhsT=rotT[:], rhs=nqn[:, 0:N], start=True, stop=True),
              reads=["nqn", "rotT_sb"], writes=[kr])
            A("gpsimd", lambda e: e.tensor_tensor(out=nt1[:, 0:N], in0=nqn[:, 0:N], in1=cos_ap, op=ALU.mult),
              reads=["nqn"] + cos_keys, writes=["nt1"])
            A("vector", lambda e: e.tensor_tensor(out=nt2[:, 0:N], in0=ps_r[:, 0:N], in1=sin_ap, op=ALU.mult),
              reads=[kr] + cos_keys, writes=["nt2"])
            A("vector", lambda e: e.tensor_tensor(out=out_ap, in0=nt1[:, 0:N], in1=nt2[:, 0:N], op=ALU.add),
              reads=["nt1", "nt2"], writes=[out_key])

        epsb = sb("epsb", [128, 1], F32)
        A("gpsimd", lambda e: e.memset(epsb[:], EPS), writes=["epsb"])

        with ExitStack() as st1:
            Wfm = sb("Wfm", [128, 8, 1152], BF16, st1)
            Wtm = sb("Wtm", [128, 8, 396], BF16, st1)
            wst = [sb(f"wst{i}", [128, 1152], F32, st1) for i in range(2)]
            xt = [sb(f"xt{i}", [128, 1024], F32, st1) for i in range(2)]
            junk = sb("junk", [128, 1024], BF16, st1)
            xn = sb("xn", [128, 1024], BF16, st1)
            ssq = sb("ssq", [128, 1], F32, st1)
            lnv1 = sb("lnv1", [128, 1], F32, st1)
            rstd1 = sb("rstd1", [128, 1], F32, st1)
            hT = [sb(f"hT{i}", [128, 8, 512], BF16, st1) for i in range(2)]
            cosb = [sb(f"cosb{i}", [128, 512], F32, st1) for i in range(2)]
            sinb = [sb(f"sinb{i}", [128, 512], F32, st1) for i in range(2)]
            stg = [sb(f"stg{i}", [128, 512], F32, st1) for i in range(2)]
            sgn = [sb(f"sgn{i}", [128, 256], F32, st1) for i in range(2)]
            tp = psb[0]
            tpb = tp[:].bitcast(BF16)
            psfm = [psb[1], psb[2]]
            pstm = psb[3]
            ps_m = psb[4]
            ps_r = psb[5]

            for k in range(8):
                s = k % 2
                A("sync", lambda e, k=k, s=s: e.dma_start(out=wst[s][:], in_=W["win_fm"][:, k, :]),
                  writes=[f"wst{s}"], dma=f"ld_wst{s}")
                A("vector", lambda e, k=k, s=s: e.tensor_scalar(out=Wfm[:, k, :], in0=wst[s][:], scalar1=par[:, k:k + 1],
                                                                 scalar2=None, op0=ALU.mult),
                  reads=[f"wst{s}", "par"], writes=["Wfm"])
            for k in range(8):
                s = k % 2
                A("sync", lambda e, k=k, s=s: e.dma_start(out=wst[s][:, 0:396], in_=W["win_tm"][:, k, :]),
                  writes=[f"wst{s}"], dma=f"ld_wst{s}")
                A("vector", lambda e, k=k, s=s: e.tensor_scalar(out=Wtm[:, k, :], in0=wst[s][:, 0:396], scalar1=par[:, k:k + 1],
                                                                 scalar2=None, op0=ALU.mult),
                  reads=[f"wst{s}", "par"], writes=["Wtm"])

            stg_i = 0
            for G in range(NG):
                hs = G % 2
                gs = slice(G * 512, (G + 1) * 512)
                A("sync", lambda e, hs=hs, gs=gs: e.dma_start(out=cosb[hs][:], in_=C["cosT"][:, gs]),
                  writes=[f"cosb{hs}"], dma=f"ld_cosb{hs}")
                A("sync", lambda e, hs=hs, gs=gs: e.dma_start(out=sinb[hs][:], in_=C["sinT"][:, gs]),
                  writes=[f"sinb{hs}"], dma=f"ld_sinb{hs}")
                for tl in range(4):
                    tt = G * 4 + tl
                    xs = tt % 2
                    A("sync", lambda e, tt=tt, xs=xs: e.dma_start(out=xt[xs][:], in_=x[tt * 128:(tt + 1) * 128, :]),
                      writes=[f"xt{xs}"], dma=f"ld_xt{xs}")
                    A("scalar", lambda e, xs=xs: e.activation(out=junk[:], in_=xt[xs][:], func=AF.Square, accum_out=ssq[:, 0:1]),
                      reads=[f"xt{xs}"], writes=["junk", "ssq"])
                    A("scalar", lambda e: e.activation(out=lnv1[:], in_=ssq[:], func=AF.Ln, bias=epsb[:, 0:1], scale=1.0 / 1024),
                      reads=["ssq", "epsb"], writes=["lnv1"])
                    A("scalar", lambda e: e.activation(out=rstd1[:], in_=lnv1[:], func=AF.Exp, scale=-0.5),
                      reads=["lnv1"], writes=["rstd1"])
                    A("vector", lambda e, xs=xs: e.tensor_scalar(out=xn[:], in0=xt[xs][:], scalar1=rstd1[:, 0:1], scalar2=None,
                                                                  op0=ALU.mult),
                      reads=[f"xt{xs}", "rstd1"], writes=["xn"])
                    for k in range(8):
                        A("tensor", lambda e, k=k: e.transpose(out=tpb[:, k * 128:(k + 1) * 128], in_=xn[:, k * 128:(k + 1) * 128],
                                                               identity=identb[:]),
                          reads=["xn", "identb"], writes=["tp"])
                    A("vector", lambda e, hs=hs, tl=tl: e.tensor_copy(out=hT[hs][:, :, tl * 128:(tl + 1) * 128],
                                                                       in_=tpb.rearrange("p (k t) -> p k t", k=8)),
                      reads=["tp"], writes=[f"hT{hs}"])
                for c in range(9):
                    pb = c % 2
                    for k in range(8):
                        A("tensor", lambda e, c=c, k=k, pb=pb, hs=hs: e.matmul(
                            psfm[pb][:], lhsT=Wfm[:, k, c * 128:(c + 1) * 128], rhs=hT[hs][:, k, :], start=(k == 0), stop=(k == 7)),
                          reads=["Wfm", f"hT{hs}"], writes=[f"psfm{pb}"])
                    if c < 4:
                        si = stg_i % 2
                        stg_i += 1
                        fn = AF.Copy if c < 2 else AF.Silu
                        dst = rgx_s if c < 2 else rgg_s
                        A("scalar", lambda e, si=si, pb=pb, fn=fn: e.activation(out=stg[si][:], in_=psfm[pb][:], func=fn),
                          reads=[f"psfm{pb}"], writes=[f"stg{si}"])
                        A("gpsimd", lambda e, si=si, dst=dst, c=c, gs=gs: e.dma_start(out=dst[c % 2, :, gs], in_=stg[si][:]),
                          reads=[f"stg{si}"], dma=f"st_stg{si}")
                    elif c < 8:
                        if c < 6:
                            out_ap, okey, gcol = qT[:, c - 4, gs], "qT", 24
                        elif c == 6:
                            out_ap, okey, gcol = KS2[:, gs], "KS2", 25
                        else:
                            out_ap, okey, gcol = KW2[:, gs], "KW2", 26
                        normrope(psfm[pb][:], 512, gcol, cosb[hs][:], sinb[hs][:], [f"cosb{hs}", f"sinb{hs}"],
                                 out_ap, okey, f"psfm{pb}", ps_m, ps_r, "ps_m", "ps_r")
                    else:
                        A("vector", lambda e, pb=pb, gs=gs: e.tensor_copy(out=KCV[:, gs], in_=psfm[pb][:]),
                          reads=[f"psfm{pb}"], writes=["KCV"])
                for tl in range(4):
                    tt = G * 4 + tl
                    sl = tt % 2
                    for k in range(8):
                        A("tensor", lambda e, k=k, tl=tl, hs=hs: e.matmul(
                            pstm[:, 0:396], lhsT=hT[hs][:, k, tl * 128:(tl + 1) * 128], rhs=Wtm[:, k, :],
                            start=(k == 0), stop=(k == 7)),
                          reads=["Wtm", f"hT{hs}"], writes=["pstm"])
                    A("vector", lambda e, tt=tt: e.tensor_copy(out=VS[:, tt, 0:64], in_=pstm[:, 0:64]),
                      reads=["pstm"], writes=["VS"])
                    A("vector", lambda e, tt=tt: e.tensor_copy(out=VW[:, tt, 0:64], in_=pstm[:, 64:128]),
                      reads=["pstm"], writes=["VW"])
                    A("scalar", lambda e, sl=sl: e.activation(out=sgn[sl][:], in_=pstm[:, 128:384], func=AF.Silu),
                      reads=["pstm"], writes=[f"sgn{sl}"])
                    A("gpsimd", lambda e, sl=sl, tt=tt: e.dma_start(out=sgn_s[tt * 128:(tt + 1) * 128, :], in_=sgn[sl][:]),
                      reads=[f"sgn{sl}"], dma=f"st_sgn{sl}")
                    A("scalar", lambda e, tt=tt: e.activation(out=gsig[:, tt, :], in_=pstm[:, 384:396], func=AF.Sigmoid),
                      reads=["pstm"], writes=["gsig"])
            P.barrier()

        if dbg:
            for nm, t, shp, dt in (("d_qT", qT, [128, 2, S], BF16), ("d_KS2", KS2, [128, S], BF16), ("d_KW2", KW2, [128, S], BF16),
                                   ("d_KCV", KCV, [128, S], BF16), ("d_VS", VS, [128, NT, 65], BF16),
                                   ("d_VW", VW, [128, NT, 65], BF16), ("d_gsig", gsig, [128, NT, 12], F32)):
                d = dram(nm, shp, dt, "ExternalOutput")
                A("sync", lambda e, d=d, t=t: e.dma_start(out=d, in_=t[:]), reads=[t.name], dma="dbg")

        if upto >= 2:
            with ExitStack() as st2:
                xin = [sb(f"xin{i}", [128, 515], F32, st2) for i in range(2)]
                gin = [sb(f"gin{i}", [128, 512], F32, st2) for i in range(2)]
                wr = sb("wr_sb", [128, 2, 128], F32, st2)
                wi = sb("wi_sb", [128, 2, 128], F32, st2)
                cy = sb("cy", [128, 512], F32, st2)
                rr = sb("rr", [128, 512], F32, st2)
                ii = sb("ii", [128, 512], F32, st2)
                aa = sb("aa", [128, 512], F32, st2)
                a2 = sb("a2", [128, 512], F32, st2)
                sq = sb("sqm", [128, 512], F32, st2)
                uu = sb("uu", [128, 512], F32, st2)
                hh = [sb(f"hh{i}", [128, 512], F32, st2) for i in range(2)]
                yab = [sb(f"yab{i}", [128, 512], BF16, st2) for i in range(2)]
                cc = sb("cc", [128, 8], F32, st2)
                onec = sb("onec", [128, 1], F32, st2)
                psr, psi = psb[0], psb[1]
                A("sync", lambda e: e.dma_start(out=wr[:], in_=W["wrbd"]), writes=["wr_sb"], dma="ld_wr")
                A("sync", lambda e: e.dma_start(out=wi[:], in_=W["wibd"]), writes=["wi_sb"], dma="ld_wi")
                A("gpsimd", lambda e: e.memset(onec[:], 1.0), writes=["onec"])
                for c in range(2):
                    A("scalar", lambda e, c=c: e.activation(out=cc[:, 0:1], in_=par[:, 22 + c:23 + c], func=AF.Exp, scale=-1.0),
                      reads=["par"], writes=["cc"])
                    A("vector", lambda e: e.tensor_scalar(out=cc[:, 1:2], in0=cc[:, 0:1], scalar1=-1.0 / 6, scalar2=1.0 / 5,
                                                          op0=ALU.mult, op1=ALU.add), reads=["cc"], writes=["cc"])
                    for coef in (-1.0 / 4, 1.0 / 3, -1.0 / 2, 1.0):
                        A("vector", lambda e, coef=coef: e.tensor_scalar(out=cc[:, 1:2], in0=cc[:, 1:2], scalar1=cc[:, 0:1],
                                                                          scalar2=coef, op0=ALU.mult, op1=ALU.add),
                          reads=["cc"], writes=["cc"])
                    A("vector", lambda e: e.tensor_scalar(out=cc[:, 2:3], in0=cc[:, 1:2], scalar1=cc[:, 0:1], scalar2=-8.0,
                                                          op0=ALU.mult, op1=ALU.mult), reads=["cc"], writes=["cc"])
                    A("vector", lambda e: e.tensor_scalar(out=cc[:, 3:4], in0=cc[:, 2:3], scalar1=2.0, scalar2=None,
                                                          op0=ALU.mult), reads=["cc"], writes=["cc"])
                    for G in range(NG):
                        s = G % 2
                        gs = slice(G * 512, (G + 1) * 512)
                        A("sync", lambda e, s=s, c=c, gs=gs: e.dma_start(out=xin[s][:, 3:515], in_=rgx_s[c, :, gs]),
                          writes=[f"xin{s}"], dma=f"ld_xin{s}")
                        A("sync", lambda e, s=s, c=c, gs=gs: e.dma_start(out=gin[s][:], in_=rgg_s[c, :, gs]),
                          writes=[f"gin{s}"], dma=f"ld_gin{s}")
                        if G == 0:
                            A("gpsimd", lambda e, s=s: e.memset(xin[s][:, 0:3], 0.0), reads=[f"xin{s}"], writes=[f"xinh{s}"])
                        else:
                            A("gpsimd", lambda e, s=s: e.tensor_copy(out=xin[s][:, 0:3], in_=xin[1 - s][:, 512:515]),
                              reads=[f"xin{1 - s}", f"xin{s}"], writes=[f"xinh{s}"])
                        A("vector", lambda e, s=s, c=c: e.tensor_scalar(out=cy[:], in0=xin[s][:, 3:515], scalar1=par[:, 11 + 4 * c:12 + 4 * c],
                                                                         scalar2=par[:, 16 + c:17 + c], op0=ALU.mult, op1=ALU.add),
                          reads=[f"xin{s}", "par"], writes=["cy"])
                        for k in range(3):
                            A("vector", lambda e, s=s, c=c, k=k: e.scalar_tensor_tensor(
                                out=cy[:], in0=xin[s][:, k:k + 512], scalar=par[:, 8 + 4 * c + k:9 + 4 * c + k], in1=cy[:],
                                op0=ALU.mult, op1=ALU.add),
                              reads=[f"xin{s}", f"xinh{s}", "par", "cy"], writes=["cy"])
                        A("tensor", lambda e, c=c: e.matmul(psr[:], lhsT=wr[:, c, :], rhs=cy[:], start=True, stop=True),
                          reads=["cy", "wr_sb"], writes=["psr"])
                        A("tensor", lambda e, c=c: e.matmul(psi[:], lhsT=wi[:, c, :], rhs=cy[:], start=True, stop=True),
                          reads=["cy", "wi_sb"], writes=["psi"])
                        A("scalar", lambda e, c=c: e.activation(out=rr[:], in_=psr[:], func=AF.Sigmoid, bias=par[:, 18 + c:19 + c], scale=1.0),
                          reads=["psr", "par"], writes=["rr"])
                        A("scalar", lambda e, c=c: e.activation(out=ii[:], in_=psi[:], func=AF.Sigmoid, bias=par[:, 20 + c:21 + c], scale=1.0),
                          reads=["psi", "par"], writes=["ii"])
                        A("scalar", lambda e: e.activation(out=aa[:], in_=rr[:], func=AF.Exp, scale=cc[:, 2:3]),
                          reads=["rr", "cc"], writes=["aa"])
                        A("scalar", lambda e: e.activation(out=a2[:], in_=rr[:], func=AF.Exp, scale=cc[:, 3:4]),
                          reads=["rr", "cc"], writes=["a2"])
                        A("scalar", lambda e: e.activation(out=sq[:], in_=a2[:], func=AF.Sqrt, bias=onec[:, 0:1], scale=-1.0),
                          reads=["a2", "onec"], writes=["sqm"])
                        A("gpsimd", lambda e: e.tensor_tensor(out=uu[:], in0=ii[:], in1=cy[:], op=ALU.mult),
                          reads=["ii", "cy"], writes=["uu"])
                        A("vector", lambda e: e.tensor_tensor(out=uu[:], in0=uu[:], in1=sq[:], op=ALU.mult),
                          reads=["uu", "sqm"], writes=["uu"])
                        init = 0.0 if G == 0 else hh[1 - s][:, 511:512]
                        A("vector", lambda e, s=s, init=init: e.tensor_tensor_scan(out=hh[s][:], data0=aa[:], data1=uu[:], initial=init,
                                                                                   op0=ALU.mult, op1=ALU.add),
                          reads=["aa", "uu", f"hh{1 - s}"], writes=[f"hh{s}"])
                        A("vector", lambda e, s=s: e.tensor_tensor(out=yab[s][:], in0=hh[s][:], in1=gin[s][:], op=ALU.mult),
                          reads=[f"hh{s}", f"gin{s}"], writes=[f"yab{s}"])
                        A("gpsimd", lambda e, s=s, c=c, gs=gs: e.dma_start(out=yT[c, :, gs], in_=yab[s][:]),
                          reads=[f"yab{s}"], dma=f"st_yab{s}")
                P.barrier()

        if upto >= 3:
            with ExitStack() as st3:
                w1st = sb("w1st", [128, 8, 128], F32, st3)
                w1b = sb("w1b", [128, 32, 128], BF16, st3)
                pest = sb("pest", [128, 32], F32, st3)
                peb = sb("peb", [128, 32], BF16, st3)
                w2st = sb("w2st", [128, 192], F32, st3)
                w2b = sb("w2b", [128, 192], BF16, st3)
                cvec = sb("cvec", [128, 2], F32, st3)
                h1s = [sb(f"h1s{i}", [128, 512], BF16, st3) for i in range(2)]
                cosc = sb("cosc", [128, 512], F32, st3)
                sinc = sb("sinc", [128, 512], F32, st3)
                ps_cv, ps_h1, ps_kc, ps_vc, ps_m, ps_r = psb[0], psb[1], psb[2], psb[3], psb[4], psb[5]
                for q4 in range(4):
                    A("sync", lambda e, q4=q4: e.dma_start(out=w1st[:], in_=W["w1kv"][:, q4 * 8:(q4 + 1) * 8, :]),
                      writes=["w1st"], dma="ld_w1st")
                    A("vector", lambda e, q4=q4: e.tensor_copy(out=w1b[:, q4 * 8:(q4 + 1) * 8, :], in_=w1st[:]),
                      reads=["w1st"], writes=["w1b"])
                A("sync", lambda e: e.dma_start(out=pest[:], in_=W["pekv"]), writes=["pest"], dma="ld_pest")
                A("vector", lambda e: e.tensor_copy(out=peb[:], in_=pest[:]), reads=["pest"], writes=["peb"])
                A("sync", lambda e: e.dma_start(out=w2st[:, 0:128], in_=W["w2k2"]), writes=["w2sta"], dma="ld_w2a")
                A("sync", lambda e: e.dma_start(out=w2st[:, 128:192], in_=W["w2v"]), writes=["w2stb"], dma="ld_w2b")
                A("vector", lambda e: e.tensor_copy(out=w2b[:], in_=w2st[:]), reads=["w2sta", "w2stb"], writes=["w2b"])
                A("sync", lambda e: e.dma_start(out=cosc[:], in_=C["cosC"]), writes=["cosc"], dma="ld_cosc")
                A("sync", lambda e: e.dma_start(out=sinc[:], in_=C["sinC"]), writes=["sinc"], dma="ld_sinc")
                for half in range(2):
                    rows = slice(64 * half, 64 * half + 64)
                    A("gpsimd", lambda e, half=half: e.memset(h1s[half][:], 0.0), writes=[f"h1s{half}"])
                    for p in range(32):
                        A("tensor", lambda e, p=p, rows=rows, half=half: e.matmul(
                            ps_cv[:, half:half + 1], lhsT=w1b[rows, p, :], rhs=peb[rows, p:p + 1], start=(p == 0), stop=(p == 31)),
                          reads=["w1b", "peb"], writes=["ps_cv"])
                    A("vector", lambda e, half=half: e.tensor_copy(out=cvec[:, half:half + 1], in_=ps_cv[:, half:half + 1]),
                      reads=["ps_cv"], writes=["cvec"])
                    for p in range(32):
                        A("tensor", lambda e, p=p, rows=rows: e.matmul(
                            ps_h1[:, 0:NCMP], lhsT=w1b[rows, p, :], rhs=KCV[rows, p:p + 16 * (NCMP - 1) + 1:16],
                            start=(p == 0), stop=(p == 31)),
                          reads=["w1b", "KCV"], writes=["ps_h1"])
                    A("scalar", lambda e, half=half: e.activation(out=h1s[half][:, 0:NCMP], in_=ps_h1[:, 0:NCMP], func=AF.Silu,
                                                                 bias=cvec[:, half:half + 1], scale=1.0),
                      reads=["ps_h1", "cvec"], writes=[f"h1s{half}"])
                A("tensor", lambda e: e.matmul(ps_kc[:, 0:NCMP], lhsT=w2b[:, 0:128], rhs=h1s[0][:, 0:NCMP], start=True, stop=True),
                  reads=["w2b", "h1s0"], writes=["ps_kc"])
                normrope(ps_kc[:, 0:NCMP], NCMP, 27, cosc[:, 0:NCMP], sinc[:, 0:NCMP], ["cosc", "sinc"],
                         KC2[:, 0:NCMP], "KC2", "ps_kc", ps_m, ps_r, "ps_m", "ps_r")
                for kc in range(NCT):
                    A("tensor", lambda e, kc=kc: e.matmul(ps_vc[:, 0:64], lhsT=h1s[1][:, kc * 128:(kc + 1) * 128], rhs=w2b[:, 128:192],
                                                          start=True, stop=True),
                      reads=["w2b", "h1s1"], writes=["ps_vc"])
                    A("vector", lambda e, kc=kc: e.tensor_copy(out=VC[:, kc, 0:64], in_=ps_vc[:, 0:64]),
                      reads=["ps_vc"], writes=["VC"])
                P.barrier()
            if dbg:
                for nm, t, shp, dt in (("d_KC2", KC2, [128, 512], BF16), ("d_VC", VC, [128, NCT, 65], BF16)):
                    d = dram(nm, shp, dt, "ExternalOutput")
                    A("sync", lambda e, d=d, t=t: e.dma_start(out=d, in_=t[:]), reads=[t.name], dma="dbg")

        if upto >= 4:
            with ExitStack() as st4:
                EE = sb("EE_sb", [128, S], BF16, st4)
                OV = sb("OV_sb", [128, NCT, 128], BF16, st4)
                TRIC = sb("TRIC_sb", [128, 256], BF16, st4)
                TRIW = sb("TRIW_sb", [128, 256], BF16, st4)
                CMASK = sb("CMASK_sb", [128, 17, 256], BF16, st4)
                FORCEP = sb("FORCEP_sb", [128, 256], F32, st4)
                for nm, t in (("EE", EE), ("OV", OV), ("TRIC", TRIC), ("TRIW", TRIW), ("CMASK", CMASK), ("FORCEP", FORCEP)):
                    A("sync", lambda e, t=t, nm=nm: e.dma_start(out=t[:], in_=C[nm]), writes=[nm], dma="ld_" + nm)
                PT = [sb(f"PT{i}", [128, 512], BF16, st4) for i in range(2)]
                sgt = [sb(f"sgt{i}", [128, 256], F32, st4) for i in range(2)]
                Osb = [sb(f"Osb{i}", [128, 260], F32, st4) for i in range(3)]
                den = sb("den", [128, 4], F32, st4)
                rden = sb("rden", [128, 4], F32, st4)
                coef = sb("coef", [128, 4], F32, st4)
                imp = sb("imp", [128, 128], F32, st4)
                imp2 = sb("imp2", [128, 128], F32, st4)
                m8a = sb("m8a", [128, 8], F32, st4)
                m8b = sb("m8b", [128, 8], F32, st4)
                negm = sb("negm", [128, 128], BF16, st4)
                negmT = sb("negmT", [128, 256], BF16, st4)
                yacc = sb("yacc", [128, 256], F32, st4)
                ybb = sb("ybb", [128, 256], BF16, st4)
                ybT = [sb(f"ybT{i}", [128, 256], BF16, st4) for i in range(2)]
                SA = [psb[0], psb[2]]
                SB = [psb[1], psb[3]]
                ACC = [psb[4], psb[5]]
                U = psb[6]
                TPb = psb[7][:].bitcast(BF16)
                cnt = {"s": 0, "a": 0}

                def score_tile(Kt, kslice, qlo, qhi, mask_l, mask_r, mask_keys, Vt, vidx, vkey, acc, akey, first, last, extra=None):
                    s = cnt["s"] % 2
                    cnt["s"] += 1
                    hasm = mask_l is not None
                    A("tensor", lambda e: e.matmul(SA[s][:, 0:256], lhsT=Kt[0:64, kslice], rhs=qlo, start=True, stop=not hasm),
                      reads=[Kt.name, "qT"], writes=[f"SA{s}"])
                    A("tensor", lambda e: e.matmul(SB[s][:, 0:256], lhsT=Kt[64:128, kslice], rhs=qhi, start=True, stop=not hasm),
                      reads=[Kt.name, "qT"], writes=[f"SB{s}"])
                    if hasm:
                        A("tensor", lambda e: e.matmul(SA[s][:, 0:256], lhsT=mask_l, rhs=mask_r, start=False, stop=True),
                          reads=mask_keys, writes=[f"SA{s}"])
                        A("tensor", lambda e: e.matmul(SB[s][:, 0:256], lhsT=mask_l, rhs=mask_r, start=False, stop=True),
                          reads=mask_keys, writes=[f"SB{s}"])
                    A("scalar", lambda e: e.activation(out=PT[s][:, 0:256], in_=SA[s][:, 0:256], func=AF.Exp, scale=0.125),
                      reads=[f"SA{s}"], writes=[f"PT{s}"])
                    A("scalar", lambda e: e.activation(out=PT[s][:, 256:512], in_=SB[s][:, 0:256], func=AF.Exp, scale=0.125),
                      reads=[f"SB{s}"], writes=[f"PT{s}"])
                    for h in range(4):
                        A("tensor", lambda e, h=h: e.matmul(acc[:, h * 65:(h + 1) * 65], lhsT=PT[s][:, h * 128:(h + 1) * 128],
                                                            rhs=Vt[:, vidx, :], start=(first and h == 0), stop=last,
                                                            skip_group_check=True),
                          reads=[f"PT{s}", vkey], writes=[akey])
                        if extra is not None:
                            A("tensor", lambda e, h=h: e.matmul(U[:, h * 128:(h + 1) * 128], lhsT=PT[s][:, h * 128:(h + 1) * 128],
                                                                rhs=OV[:, extra, :], start=(first and h == 0), stop=last,
                                                                skip_group_check=True),
                              reads=[f"PT{s}", "OV"], writes=["U"])

                for qi in range(NT):
                    ts_ = slice(qi * 128, (qi + 1) * 128)
                    qlo = qT[0:64, :, ts_]
                    qhi = qT[64:128, :, ts_]
                    sg = qi % 2
                    A("sync", lambda e, sg=sg, ts_=ts_: e.dma_start(out=sgt[sg][:], in_=sgn_s[ts_, :]),
                      writes=[f"sgt{sg}"], dma=f"ld_sgt{sg}")
                    kcs = [kc for kc in range(NCT) if qi - 16 * kc >= 0]
                    acc_i = cnt["a"] % 2
                    acc = ACC[acc_i]
                    akey = f"ACC{acc_i}"
                    cnt["a"] += 1
                    for i, kc in enumerate(kcs):
                        r = qi - 16 * kc
                        ml, mr = (identb[:], CMASK[:, r, :]) if r <= 16 else (None, None)
                        score_tile(KC2, slice(kc * 128, (kc + 1) * 128), qlo, qhi, ml, mr, ["identb", "CMASK"],
                                   VC, kc, "VC", acc, akey, i == 0, i == len(kcs) - 1, extra=kc)
                    A("vector", lambda e, acc=acc: e.tensor_copy(out=Osb[0][:], in_=acc[:, 0:260]), reads=[akey], writes=["Osb0"])
                    A("vector", lambda e: e.tensor_scalar(out=den[:], in0=Osb[0][:, 64:260:65], scalar1=1e-30, scalar2=None, op0=ALU.max),
                      reads=["Osb0"], writes=["den"])
                    A("vector", lambda e: e.reciprocal(out=rden[:], in_=den[:]), reads=["den"], writes=["rden"])
                    A("vector", lambda e: e.tensor_scalar(out=imp[:], in0=U[:, 0:128], scalar1=rden[:, 0:1], scalar2=None, op0=ALU.mult),
                      reads=["U", "rden"], writes=["imp"])
                    for h in range(1, 4):
                        A("vector", lambda e, h=h: e.scalar_tensor_tensor(out=imp[:], in0=U[:, h * 128:(h + 1) * 128],
                                                                          scalar=rden[:, h:h + 1], in1=imp[:], op0=ALU.mult, op1=ALU.add),
                          reads=["U", "rden", "imp"], writes=["imp"])
                    A("vector", lambda e, qi=qi: e.tensor_tensor(out=imp[:], in0=imp[:], in1=FORCEP[:, 128 - 2 * qi:256 - 2 * qi], op=ALU.add),
                      reads=["imp", "FORCEP"], writes=["imp"])
                    A("vector", lambda e: e.memset(imp[:, 0:1], 4e30), reads=["imp"], writes=["imp"])
                    A("vector", lambda e: e.max(out=m8a[:], in_=imp[:]), reads=["imp"], writes=["m8a"])
                    A("vector", lambda e: e.match_replace(out=imp2[:], in_to_replace=m8a[:], in_values=imp[:], imm_value=-3e38),
                      reads=["imp", "m8a"], writes=["imp2"])
                    A("vector", lambda e: e.max(out=m8b[:], in_=imp2[:]), reads=["imp2"], writes=["m8b"])
                    A("vector", lambda e: e.tensor_scalar(out=negm[:], in0=imp[:], scalar1=m8b[:, 7:8], scalar2=NEGM,
                                                          op0=ALU.is_lt, op1=ALU.mult),
                      reads=["imp", "m8b"], writes=["negm"])
                    A("tensor", lambda e: e.transpose(out=TPb[:, 0:128], in_=negm[:], identity=identb[:]),
                      reads=["negm", "identb"], writes=["TPb"])
                    A("vector", lambda e: e.tensor_copy(out=negmT[:, 0:128], in_=TPb[:, 0:128]), reads=["TPb"], writes=["negmT"])
                    A("vector", lambda e: e.tensor_copy(out=negmT[:, 128:256], in_=TPb[:, 0:128]), reads=["TPb"], writes=["negmT"])
                    acc_i = cnt["a"] % 2
                    acc = ACC[acc_i]
                    akey = f"ACC{acc_i}"
                    cnt["a"] += 1
                    for kt in range(qi + 1):
                        if kt < qi:
                            ml, mr, mk = EE[:, kt * 128:(kt + 1) * 128], negmT[:], ["EE", "negmT"]
                        else:
                            ml, mr, mk = identb[:], TRIC[:], ["identb", "TRIC"]
                        score_tile(KS2, slice(kt * 128, (kt + 1) * 128), qlo, qhi, ml, mr, mk, VS, kt, "VS", acc, akey, kt == 0, kt == qi)
                    A("vector", lambda e, acc=acc: e.tensor_copy(out=Osb[1][:], in_=acc[:, 0:260]), reads=[akey], writes=["Osb1"])
                    acc_i = cnt["a"] % 2
                    acc = ACC[acc_i]
                    akey = f"ACC{acc_i}"
                    cnt["a"] += 1
                    k0 = max(0, qi - 4)
                    for kt in range(k0, qi + 1):
                        if kt == qi:
                            ml, mr, mk = identb[:], TRIC[:], ["identb", "TRIC"]
                        elif kt == qi - 4:
                            ml, mr, mk = identb[:], TRIW[:], ["identb", "TRIW"]
                        else:
                            ml, mr, mk = None, None, []
                        score_tile(KW2, slice(kt * 128, (kt + 1) * 128), qlo, qhi, ml, mr, mk, VW, kt, "VW", acc, akey, kt == k0, kt == qi)
                    A("vector", lambda e, acc=acc: e.tensor_copy(out=Osb[2][:], in_=acc[:, 0:260]), reads=[akey], writes=["Osb2"])
                    for br in range(3):
                        if br > 0:
                            A("vector", lambda e, br=br: e.tensor_scalar(out=den[:], in0=Osb[br][:, 64:260:65], scalar1=1e-30, scalar2=None,
                                                                         op0=ALU.max), reads=[f"Osb{br}"], writes=["den"])
                            A("vector", lambda e: e.reciprocal(out=rden[:], in_=den[:]), reads=["den"], writes=["rden"])
                        A("vector", lambda e, br=br, qi=qi: e.tensor_tensor(out=coef[:], in0=rden[:], in1=gsig[:, qi, br:12:3], op=ALU.mult),
                          reads=["rden", "gsig"], writes=["coef"])
                        for h in range(4):
                            if br == 0:
                                A("vector", lambda e, h=h: e.tensor_scalar(out=yacc[:, h * 64:(h + 1) * 64], in0=Osb[0][:, h * 65:h * 65 + 64],
                                                                           scalar1=coef[:, h:h + 1], scalar2=None, op0=ALU.mult),
                                  reads=["Osb0", "coef"], writes=["yacc"])
                            else:
                                A("vector", lambda e, h=h, br=br: e.scalar_tensor_tensor(
                                    out=yacc[:, h * 64:(h + 1) * 64], in0=Osb[br][:, h * 65:h * 65 + 64], scalar=coef[:, h:h + 1],
                                    in1=yacc[:, h * 64:(h + 1) * 64], op0=ALU.mult, op1=ALU.add),
                                  reads=[f"Osb{br}", "coef", "yacc"], writes=["yacc"])
                    A("vector", lambda e, sg=sg: e.tensor_tensor(out=ybb[:], in0=yacc[:], in1=sgt[sg][:], op=ALU.mult),
                      reads=["yacc", f"sgt{sg}"], writes=["ybb"])
                    for j in range(2):
                        A("tensor", lambda e, j=j: e.transpose(out=TPb[:, 256 + j * 128:256 + (j + 1) * 128], in_=ybb[:, j * 128:(j + 1) * 128],
                                                               identity=identb[:]),
                          reads=["ybb", "identb"], writes=["TPy"])
                    yo = qi % 2
                    A("vector", lambda e, yo=yo: e.tensor_copy(out=ybT[yo][:], in_=TPb[:, 256:512]), reads=["TPy"], writes=[f"ybT{yo}"])
                    A("gpsimd", lambda e, yo=yo, ts_=ts_: e.dma_start(out=yT[2:4, :, ts_].rearrange("j p t -> p j t"),
                                                                      in_=ybT[yo][:].rearrange("p (j t) -> p j t", j=2)),
                      reads=[f"ybT{yo}"], dma=f"st_ybT{yo}")
                P.barrier()
        P.barrier()
        P.emit()
    return nc


def build_B(T):
    NT = T // 128
    nc = bass.Bass("TRN2", target_bir_lowering=False)
    with ExitStack() as stack:
        P = Prog(nc, stack)
        A = P.add
        x = nc.dram_tensor("x", [T, 1024], F32, kind="ExternalInput").ap()
        ycat = nc.dram_tensor("ycat", [128, 8, T], BF16, kind="ExternalInput").ap()
        wout = nc.dram_tensor("wout", [128, 8, 1024], F32, kind="ExternalInput").ap()
        xo = nc.dram_tensor("xo", [T, 1024], F32, kind="ExternalOutput").ap()

        def sb(name, shape, dt):
            return stack.enter_context(nc.sbuf_tensor(name, list(shape), dt))

        wb = sb("wb", [128, 8, 1024], BF16)
        wst = [sb(f"wst{i}", [128, 1024], F32) for i in range(2)]
        yt = [sb(f"yt{i}", [128, 8, 128], BF16) for i in range(2)]
        xt = [sb(f"xt{i}", [128, 1024], F32) for i in range(2)]
        ot = [sb(f"ot{i}", [128, 1024], F32) for i in range(2)]
        ps = [stack.enter_context(nc.psum_tensor(f"ps{i}", [128, 512], F32)) for i in range(4)]
        for k in range(8):
            s = k % 2
            A("sync", lambda e, k=k, s=s: e.dma_start(out=wst[s][:], in_=wout[:, k, :]), writes=[f"wst{s}"], dma=f"ld_wst{s}")
            A("vector", lambda e, k=k, s=s: e.tensor_copy(out=wb[:, k, :], in_=wst[s][:]), reads=[f"wst{s}"], writes=["wb"])
        for tt in range(NT):
            s = tt % 2
            ts_ = slice(tt * 128, (tt + 1) * 128)
            A("sync", lambda e, s=s, ts_=ts_: e.dma_start(out=yt[s][:], in_=ycat[:, :, ts_]), writes=[f"yt{s}"], dma=f"ld_yt{s}")
            A("sync", lambda e, s=s, ts_=ts_: e.dma_start(out=xt[s][:], in_=x[ts_, :]), writes=[f"xt{s}"], dma=f"ld_xt{s}")
            for n in range(2):
                pi = (2 * tt + n) % 4
                for k in range(8):
                    A("tensor", lambda e, k=k, n=n, s=s, pi=pi: e.matmul(ps[pi][:], lhsT=yt[s][:, k, :], rhs=wb[:, k, n * 512:(n + 1) * 512],
                                                                          start=(k == 0), stop=(k == 7)),
                      reads=[f"yt{s}", "wb"], writes=[f"ps{pi}"])
                A("vector", lambda e, n=n, s=s, pi=pi: e.tensor_tensor(out=ot[s][:, n * 512:(n + 1) * 512], in0=ps[pi][:],
                                                                        in1=xt[s][:, n * 512:(n + 1) * 512], op=ALU.add),
                  reads=[f"ps{pi}", f"xt{s}"], writes=[f"ot{s}"])
            A("gpsimd", lambda e, s=s, ts_=ts_: e.dma_start(out=xo[ts_, :], in_=ot[s][:]), reads=[f"ot{s}"], dma=f"st_ot{s}")
        P.barrier()
        P.emit()
    return nc


_CACHE = {}


def _get(name, fn):
    if name not in _CACHE:
        _CACHE[name] = fn()
    return _CACHE[name]


def kernel(**inp):
    inp = {k: np.asarray(v) for k, v in inp.items()}
    x = np.ascontiguousarray(inp["x"], dtype=np.float32)
    B, S, D = x.shape
    depth = inp["w_in"].shape[0]
    consts = _get(("c", S), lambda: make_consts(S))
    ncA = _get(("A", S), lambda: build_A(S))
    ncB = _get(("B", S // 2), lambda: build_B(S // 2))
    cores = list(range(8))
    for l in range(depth):
        maps = []
        for cid in cores:
            b, g = cid // 2, cid % 2
            m = {"x": x[b]}
            m.update(prep_layer(inp, l, g))
            m.update(consts)
            maps.append(m)
        resA = run_bass_kernel_spmd(ncA, maps, core_ids=cores)
        yTs = [np.asarray(r["yT"]) for r in resA.results]
        wout = prep_wout(inp, l)
        maps = []
        H = S // 2
        for cid in cores:
            b, g = cid // 2, cid % 2
            ycat = np.concatenate([yTs[2 * b + gg][:, :, g * H:(g + 1) * H] for gg in range(2)], 0)
            maps.append({"x": np.ascontiguousarray(x[b, g * H:(g + 1) * H]),
                         "ycat": np.ascontiguousarray(ycat.transpose(1, 0, 2)), "wout": wout})
        resB = run_bass_kernel_spmd(ncB, maps, core_ids=cores)
        xn = np.empty_like(x)
        for cid in cores:
            b, g = cid // 2, cid % 2
            xn[b, g * H:(g + 1) * H] = np.asarray(resB.results[cid]["xo"])
        x = xn
    return x
```
